# Optimizing a Trainium2 kernel written in Bass

```python
import math
import jax, jax.numpy as jnp
from jax import lax
import numpy as np

D_MODEL = 1024
BATCH = 8
SEQ = 4096
DEPTH = 4

CHUNK = 64
SSM_WIDTH = D_MODEL // 4
SSM_GROUP = 16
SSM_GROUPS = SSM_WIDTH // SSM_GROUP
SSM_STATE = 64
GMLP_WIDTH = D_MODEL // 4
GMLP_HEADS = 4
GMLP_HEAD_DIM = GMLP_WIDTH // GMLP_HEADS
GMLP_WINDOW = 128
DIFF_WIDTH = D_MODEL // 2
DIFF_HEADS = 4
DIFF_VDIM = DIFF_WIDTH // DIFF_HEADS
DIFF_QK_DIM = DIFF_VDIM // 2
MIX_WIDTH = SSM_WIDTH + GMLP_WIDTH + DIFF_WIDTH
IN_COLS = SSM_WIDTH + 2 * GMLP_WIDTH + 3 * DIFF_WIDTH
D_FF = ((8 * D_MODEL // 3 + 127) // 128) * 128
CONV_WIDTH = 3
ROPE_THETA = 10000.0
Q_BLOCK = 128
EPS = 1e-6

kernel_name = "hybrid_s5_sgu_diffattn_trunk"


def rms_norm(x, g):
    xf = x.astype(jnp.float32)
    y = xf * lax.rsqrt(jnp.mean(xf * xf, axis=-1, keepdims=True) + EPS)
    return (y * g.astype(jnp.float32)).astype(x.dtype)


def rope_tables(seq):
    half = DIFF_QK_DIM // 2
    inv = ROPE_THETA ** (-jnp.arange(half, dtype=jnp.float32) / half)
    ang = jnp.arange(seq, dtype=jnp.float32)[:, None] * inv[None, :]
    ang = jnp.concatenate([ang, ang], axis=-1)
    return jnp.cos(ang), jnp.sin(ang)


def apply_rope(x, cos, sin):
    x1, x2 = jnp.split(x, 2, axis=-1)
    rot = jnp.concatenate([-x2, x1], axis=-1)
    c = cos[None, :, None, None, :]
    s = sin[None, :, None, None, :]
    return (x.astype(jnp.float32) * c + rot.astype(jnp.float32) * s).astype(x.dtype)


def s5_mixer(u, a_re, a_im, log_dt, b_re, b_im, c_re, c_im, d_skip, w_glu, b_glu):
    bsz, seq, _ = u.shape
    uf = u.astype(jnp.float32).reshape(bsz, seq, SSM_GROUPS, SSM_GROUP)
    lam = lax.complex(a_re.astype(jnp.float32), a_im.astype(jnp.float32))
    dt = jnp.exp(log_dt.astype(jnp.float32))[:, None]
    a_bar = jnp.exp(lam * dt)
    b_mat = lax.complex(b_re.astype(jnp.float32), b_im.astype(jnp.float32))
    b_bar = ((a_bar - 1.0) / lam)[..., None] * b_mat
    bu = jnp.einsum("bsgh,gph->bsgp", uf.astype(jnp.complex64), b_bar)

    def combine(left, right):
        a_l, b_l = left
        a_r, b_r = right
        return a_l * a_r, a_r * b_l + b_r

    a_seq = jnp.broadcast_to(a_bar, (1, seq) + a_bar.shape)
    _, states = lax.associative_scan(combine, (a_seq, bu), axis=1)
    c_mat = lax.complex(c_re.astype(jnp.float32), c_im.astype(jnp.float32))
    y = jnp.real(jnp.einsum("bsgp,ghp->bsgh", states, c_mat))
    y = y + d_skip.astype(jnp.float32).reshape(SSM_GROUPS, SSM_GROUP) * uf
    y = jax.nn.gelu(y.reshape(bsz, seq, SSM_WIDTH))
    y = y * jax.nn.sigmoid(y @ w_glu.astype(jnp.float32) + b_glu.astype(jnp.float32))
    return y.astype(u.dtype)


def spatial_gating(u, v, v_gain, w_s, b_s):
    bsz, seq, _ = u.shape
    v = rms_norm(v, v_gain)
    nwin = seq // GMLP_WINDOW
    v = v.reshape(bsz, nwin, GMLP_WINDOW, GMLP_HEADS, GMLP_HEAD_DIM)
    pos_chunk = jnp.arange(GMLP_WINDOW) // CHUNK
    mask = pos_chunk[None, :] <= pos_chunk[:, None]
    w = jnp.where(mask[None], w_s, 0).astype(v.dtype)
    mixed = jnp.einsum("hij,bnjhc->bnihc", w, v)
    mixed = mixed + b_s.astype(v.dtype).T[None, None, :, :, None]
    return u * mixed.reshape(bsz, seq, GMLP_WIDTH)


def diff_attention(q, k, v, q_gain, k_gain, lam_q1, lam_k1, lam_q2, lam_k2, sub_gain, lambda_init):
    bsz, seq, _ = q.shape
    q = q.reshape(bsz, seq, DIFF_HEADS, 2, DIFF_QK_DIM)
    k = k.reshape(bsz, seq, DIFF_HEADS, 2, DIFF_QK_DIM)
    v = v.reshape(bsz, seq, DIFF_HEADS, DIFF_VDIM)
    q = rms_norm(q, q_gain)
    k = rms_norm(k, k_gain)
    cos, sin = rope_tables(seq)
    q = apply_rope(q, cos, sin)
    k = apply_rope(k, cos, sin)
    lam = (jnp.exp(jnp.sum(lam_q1.astype(jnp.float32) * lam_k1.astype(jnp.float32)))
           - jnp.exp(jnp.sum(lam_q2.astype(jnp.float32) * lam_k2.astype(jnp.float32)))
           + lambda_init)
    nblk = seq // Q_BLOCK
    q_blocks = jnp.moveaxis(q.reshape(bsz, nblk, Q_BLOCK, DIFF_HEADS, 2, DIFF_QK_DIM), 1, 0)
    key_chunk = jnp.arange(seq) // CHUNK
    scale = DIFF_QK_DIM ** -0.5

    def block(args):
        qb, blk = args
        q_chunk = (blk * Q_BLOCK + jnp.arange(Q_BLOCK)) // CHUNK
        mask = key_chunk[None, :] <= q_chunk[:, None]
        s = jnp.einsum("bqhcd,bkhcd->bhcqk", qb, k).astype(jnp.float32) * scale
        s = jnp.where(mask, s, -jnp.inf)
        p = jax.nn.softmax(s, axis=-1)
        w = (p[:, :, 0] - lam * p[:, :, 1]).astype(v.dtype)
        return jnp.einsum("bhqk,bkhe->bqhe", w, v)

    out = lax.map(block, (q_blocks, jnp.arange(nblk)))
    out = jnp.moveaxis(out, 0, 1).reshape(bsz, seq, DIFF_HEADS, DIFF_VDIM)
    out = rms_norm(out, sub_gain) * (1.0 - lambda_init)
    return out.reshape(bsz, seq, DIFF_WIDTH)


def conv_gated_mlp(x, w_up, conv_w, conv_b, w_down):
    seq = x.shape[1]
    h = x @ w_up
    hp = jnp.pad(h, ((0, 0), (CONV_WIDTH - 1, 0), (0, 0)))
    acc = conv_b
    for i in range(CONV_WIDTH):
        acc = acc + conv_w[i] * hp[:, i:i + seq]
    gate, val = jnp.split(acc, 2, axis=-1)
    return (jax.nn.gelu(gate) * val) @ w_down


def setup_inputs(seed: int = 0) -> dict:
    key = jax.random.key(seed)
    ks = iter(jax.random.split(key, 40))

    def nrm(shape, std):
        return jax.random.normal(next(ks), shape, jnp.float32) * std

    L, G, P, C = DEPTH, SSM_GROUPS, SSM_STATE, SSM_GROUP
    x = nrm((BATCH, SEQ, D_MODEL), 1.0)
    attn_norm_g = 1.0 + nrm((L, D_MODEL), 0.02)
    w_in = nrm((L, D_MODEL, IN_COLS), D_MODEL ** -0.5)
    ssm_a_re = -0.5 + nrm((L, G, P), 0.01)
    ssm_a_im = jnp.pi * jnp.arange(P, dtype=jnp.float32)[None, None, :] + nrm((L, G, P), 0.01)
    ssm_log_dt = jax.random.uniform(next(ks), (L, G), jnp.float32, math.log(1e-3), math.log(1e-1))
    ssm_b_re = nrm((L, G, P, C), (2 * C) ** -0.5)
    ssm_b_im = nrm((L, G, P, C), (2 * C) ** -0.5)
    ssm_c_re = nrm((L, G, C, P), P ** -0.5)
    ssm_c_im = nrm((L, G, C, P), P ** -0.5)
    ssm_d = nrm((L, SSM_WIDTH), 1.0)
    ssm_w_glu = nrm((L, SSM_WIDTH, SSM_WIDTH), SSM_WIDTH ** -0.5)
    ssm_b_glu = nrm((L, SSM_WIDTH), 0.02)
    gmlp_v_g = 1.0 + nrm((L, GMLP_WIDTH), 0.02)
    gmlp_w_s = nrm((L, GMLP_HEADS, GMLP_WINDOW, GMLP_WINDOW), GMLP_WINDOW ** -0.5)
    gmlp_b_s = 1.0 + nrm((L, GMLP_HEADS, GMLP_WINDOW), 0.02)
    q_norm_g = 1.0 + nrm((L, DIFF_QK_DIM), 0.02)
    k_norm_g = 1.0 + nrm((L, DIFF_QK_DIM), 0.02)
    lambda_q1 = nrm((L, DIFF_QK_DIM), 0.1)
    lambda_k1 = nrm((L, DIFF_QK_DIM), 0.1)
    lambda_q2 = nrm((L, DIFF_QK_DIM), 0.1)
    lambda_k2 = nrm((L, DIFF_QK_DIM), 0.1)
    subln_g = 1.0 + nrm((L, DIFF_VDIM), 0.02)
    w_out = nrm((L, MIX_WIDTH, D_MODEL), MIX_WIDTH ** -0.5)
    ffn_norm_g = 1.0 + nrm((L, D_MODEL), 0.02)
    w_up = nrm((L, D_MODEL, 2 * D_FF), D_MODEL ** -0.5)
    conv_w = nrm((L, CONV_WIDTH, 2 * D_FF), CONV_WIDTH ** -0.5)
    conv_b = nrm((L, 2 * D_FF), 0.02)
    w_down = nrm((L, D_FF, D_MODEL), D_FF ** -0.5)
    return {"x": x, "attn_norm_g": attn_norm_g, "w_in": w_in,
            "ssm_a_re": ssm_a_re, "ssm_a_im": ssm_a_im, "ssm_log_dt": ssm_log_dt,
            "ssm_b_re": ssm_b_re, "ssm_b_im": ssm_b_im, "ssm_c_re": ssm_c_re, "ssm_c_im": ssm_c_im,
            "ssm_d": ssm_d, "ssm_w_glu": ssm_w_glu, "ssm_b_glu": ssm_b_glu,
            "gmlp_v_g": gmlp_v_g, "gmlp_w_s": gmlp_w_s, "gmlp_b_s": gmlp_b_s,
            "q_norm_g": q_norm_g, "k_norm_g": k_norm_g,
            "lambda_q1": lambda_q1, "lambda_k1": lambda_k1, "lambda_q2": lambda_q2, "lambda_k2": lambda_k2,
            "subln_g": subln_g, "w_out": w_out, "ffn_norm_g": ffn_norm_g,
            "w_up": w_up, "conv_w": conv_w, "conv_b": conv_b, "w_down": w_down}


def reference(x, attn_norm_g, w_in, ssm_a_re, ssm_a_im, ssm_log_dt, ssm_b_re, ssm_b_im,
              ssm_c_re, ssm_c_im, ssm_d, ssm_w_glu, ssm_b_glu, gmlp_v_g, gmlp_w_s, gmlp_b_s,
              q_norm_g, k_norm_g, lambda_q1, lambda_k1, lambda_q2, lambda_k2, subln_g, w_out,
              ffn_norm_g, w_up, conv_w, conv_b, w_down):
    o1 = SSM_WIDTH
    o2 = o1 + GMLP_WIDTH
    o3 = o2 + GMLP_WIDTH
    o4 = o3 + DIFF_WIDTH
    o5 = o4 + DIFF_WIDTH
    h = x
    for layer in range(DEPTH):
        lambda_init = 0.8 - 0.6 * math.exp(-0.3 * layer)
        xn = rms_norm(h, attn_norm_g[layer])
        proj = xn @ w_in[layer]
        u_ssm, u_g, v_g, q, k, v = jnp.split(proj, [o1, o2, o3, o4, o5], axis=-1)
        y_ssm = s5_mixer(u_ssm, ssm_a_re[layer], ssm_a_im[layer], ssm_log_dt[layer],
                         ssm_b_re[layer], ssm_b_im[layer], ssm_c_re[layer], ssm_c_im[layer],
                         ssm_d[layer], ssm_w_glu[layer], ssm_b_glu[layer])
        y_sgu = spatial_gating(u_g, v_g, gmlp_v_g[layer], gmlp_w_s[layer], gmlp_b_s[layer])
        y_diff = diff_attention(q, k, v, q_norm_g[layer], k_norm_g[layer],
                                lambda_q1[layer], lambda_k1[layer], lambda_q2[layer], lambda_k2[layer],
                                subln_g[layer], lambda_init)
        mixed = jnp.concatenate([y_ssm, y_sgu, y_diff], axis=-1)
        h = h + mixed @ w_out[layer]
        xn = rms_norm(h, ffn_norm_g[layer])
        h = h + conv_gated_mlp(xn, w_up[layer], conv_w[layer], conv_b[layer], w_down[layer])
    return h
```

```python
import math
import numpy as np
import ml_dtypes
from contextlib import ExitStack
import concourse.bass as bass
import concourse.mybir as mybir
from concourse.bass_utils import run_bass_kernel_spmd

F32 = mybir.dt.float32
BF16 = mybir.dt.bfloat16
I32 = mybir.dt.int32
AF = mybir.ActivationFunctionType
ALU = mybir.AluOpType

S = 4096
D = 1024
NL = 4
NT = 32
NBLK = 8
INC = 2304
DFF = 2816
NFT = 22
EPS = 1e-6
TWO_PI = 2.0 * math.pi
CW1 = 6.28125
CW2 = TWO_PI - CW1


class _Op:
    __slots__ = ("eng", "fn", "deps", "is_dma", "semkey", "need_inc", "count", "idx")


class Phase:
    def __init__(self, nc, name):
        self.nc = nc
        self.name = name
        self.ops = []
        self.last_writer = {}
        self.readers = {}
        self.stack = ExitStack()
        self.groups = set()

    def sbuf(self, name, shape, dtype):
        return self.stack.enter_context(self.nc.sbuf_tensor(self.name + name, list(shape), dtype))

    def psum(self, name, shape, dtype):
        return self.stack.enter_context(self.nc.psum_tensor(self.name + name, list(shape), dtype))

    def _add(self, eng, fn, reads, writes, is_dma=False, semkey=None):
        o = _Op()
        o.eng = eng
        o.fn = fn
        o.is_dma = is_dma
        o.semkey = semkey
        o.need_inc = is_dma
        o.count = 0
        o.idx = len(self.ops)
        deps = []
        for r in reads:
            w = self.last_writer.get(r)
            if w is not None:
                deps.append(w)
        for r in writes:
            w = self.last_writer.get(r)
            if w is not None:
                deps.append(w)
            deps.extend(self.readers.get(r, ()))
        o.deps = deps
        for d in deps:
            d.need_inc = True
        for r in reads:
            self.readers.setdefault(r, []).append(o)
        for r in writes:
            self.last_writer[r] = o
            self.readers[r] = []
        self.ops.append(o)
        return o

    def op(self, eng, fn, reads=(), writes=()):
        return self._add(eng, fn, tuple(reads), tuple(writes))

    def dma(self, eng, out, in_, reads=(), writes=(), semkey=None, group=False, **kw):
        assert semkey is not None
        if group:
            self.groups.add(semkey)
        return self._add(eng, lambda e: e.dma_start(out=out, in_=in_, **kw), tuple(reads),
                         tuple(writes), is_dma=True, semkey=semkey)

    def emit(self):
        nc = self.nc
        engs = ["pe", "act", "dve", "pool", "sp"]
        per_eng = {e: [o for o in self.ops if o.eng == e] for e in engs}
        for e in engs:
            if per_eng[e]:
                per_eng[e][-1].need_inc = True
        semkeys = []
        cnt = {}
        for o in self.ops:
            if not o.need_inc:
                continue
            k = o.semkey if o.is_dma else ("eng", o.eng)
            if k not in cnt:
                cnt[k] = 0
                semkeys.append(k)
            cnt[k] += 16 if o.is_dma else 1
            o.count = cnt[k]
        final = dict(cnt)
        for o in self.ops:
            if o.is_dma and o.semkey in self.groups:
                o.count = final[o.semkey]
        sems = {}
        for i, k in enumerate(semkeys):
            sems[k] = nc.alloc_semaphore(f"{self.name}_s{i}")

        def body(ename):
            def run(eng):
                waited = {}
                for o in per_eng[ename]:
                    for d in o.deps:
                        if d.is_dma:
                            k = d.semkey
                        else:
                            if d.eng == ename and ename == "pe":
                                continue
                            k = ("eng", d.eng)
                        if waited.get(k, 0) >= d.count:
                            continue
                        eng.wait_ge(sems[k], d.count)
                        waited[k] = d.count
                    ins = o.fn(eng)
                    if o.need_inc:
                        k = o.semkey if o.is_dma else ("eng", o.eng)
                        ins.then_inc(sems[k], 16 if o.is_dma else 1)
                for k in semkeys:
                    if waited.get(k, 0) >= final[k]:
                        continue
                    eng.wait_ge(sems[k], final[k])
            return run

        with nc.Block() as block:
            block.sync(body("sp"))
            block.tensor(body("pe"))
            block.scalar(body("act"))
            block.vector(body("dve"))
            block.gpsimd(body("pool"))
        nc.all_engine_barrier()
        nc.clear_and_free_semaphores(list(sems.values()))
        nc.all_engine_barrier()
        self.stack.close()


class RR:
    def __init__(self, ph, name, n, shape, dtype, psum=False):
        mk = ph.psum if psum else ph.sbuf
        self.tiles = [mk(f"{name}{i}", shape, dtype) for i in range(n)]
        self.names = [f"{name}{i}" for i in range(n)]
        self.i = 0

    def next(self):
        t, n = self.tiles[self.i], self.names[self.i]
        self.i = (self.i + 1) % len(self.tiles)
        return t, n


def ACT(ph, out, in_, func, reads, writes, **kw):
    ph.op("act", lambda e: e.activation(out=out, in_=in_, func=func, **kw), reads, writes)


def MM(ph, out, lhsT, rhs, start, stop, reads, writes):
    ph.op("pe", lambda e: e.matmul(out, lhsT, rhs, start=start, stop=stop), reads, writes)


def TR(ph, out, in_, ident, reads, writes):
    ph.op("pe", lambda e: e.transpose(out, in_, ident), reads, writes)


def TT(ph, eng, out, a, b, op, reads, writes):
    ph.op(eng, lambda e: e.tensor_tensor(out, a, b, op), reads, writes)


def STT(ph, out, in0, scalar, in1, op0, op1, reads, writes):
    ph.op("dve", lambda e: e.scalar_tensor_tensor(out, in0, scalar, in1, op0, op1), reads, writes)


def TS(ph, eng, out, in0, s1, s2, op0, op1, reads, writes):
    if op1 is None:
        ph.op(eng, lambda e: e.tensor_scalar(out, in0, s1, None, op0), reads, writes)
    else:
        ph.op(eng, lambda e: e.tensor_scalar(out, in0, s1, s2, op0, op1), reads, writes)


def CP(ph, eng, out, in_, reads, writes):
    if eng == "act":
        ph.op("act", lambda e: e.activation(out=out, in_=in_, func=AF.Copy), reads, writes)
    else:
        ph.op(eng, lambda e: e.tensor_copy(out, in_), reads, writes)


def RECIP(ph, out, in_, reads, writes):
    ph.op("dve", lambda e: e.reciprocal(out, in_), reads, writes)


def bcast_rows(ap_row, nparts, n):
    return bass.AP(ap_row.tensor, ap_row.offset, [[0, nparts], [1, n]])


def col_bcast(ap_col, n):
    return ap_col.to_broadcast([ap_col.shape[0], n])


def declare(nc, debug):
    T = {}

    def inp(name, shape, dt=F32):
        T[name] = nc.dram_tensor(name, list(shape), dt, kind="ExternalInput").ap()

    def scr(name, shape, dt, dbg=False):
        kind = "ExternalOutput" if (dbg and debug) else "Internal"
        T[name] = nc.dram_tensor(name, list(shape), dt, kind=kind).ap()

    inp("x", [S, D])
    inp("w_in", [NL, D, INC]); inp("w_out", [NL, D, D]); inp("w_up", [NL, D, 2 * DFF]); inp("w_down", [NL, DFF, D])
    inp("w_glu", [NL, 256, 256])
    inp("g1", [NL, D]); inp("g2", [NL, D])
    inp("a_re_c", [NL, 128, 8]); inp("a_im_c", [NL, 128, 8]); inp("logdt_c", [NL, 128, 8])
    inp("bt_re", [NL, 8, 128, 128]); inp("bt_im", [NL, 8, 128, 128])
    inp("ct_re", [NL, 8, 128, 128]); inp("ct_im", [NL, 8, 128, 128])
    inp("d_col", [NL, 128, 2]); inp("bglu_col", [NL, 128, 2])
    inp("vgain_col", [NL, 128, 2]); inp("wst", [NL, 4, 128, 128]); inp("bs_b", [NL, 2, 128, 128])
    inp("qk_cols", [NL, 128, 4])
    inp("lam_rows", [NL, 4, 64])
    inp("subln_col", [NL, 128, 1])
    inp("cw_col", [NL, 128, 44, 3]); inp("cb_col", [NL, 128, 44])
    inp("c_ident", [128, 128], BF16); inp("c_rm", [128, 128], BF16); inp("c_mblk", [128, 128], BF16)
    inp("c_ones", [128, 128], BF16); inp("c_mask", [128, 128], F32); inp("c_maskb", [128, 128], BF16)
    inp("c_cos", [128, S]); inp("c_sin", [128, S]); inp("c_iota", [128, 513])
    T["out"] = nc.dram_tensor("out", [S, D], F32, kind="ExternalOutput").ap()
    scr("wbf_in", [NL, D, INC], BF16); scr("wbf_out", [NL, D, D], BF16)
    scr("wbf_up", [NL, D, 2 * DFF], BF16); scr("wbf_down", [NL, DFF, D], BF16); scr("wbf_glu", [NL, 256, 256], BF16)
    scr("qT_d", [4, 128, S], BF16, True); scr("kT_d", [4, 128, S], BF16, True); scr("V_d", [S, 512], BF16, True)
    scr("mixT_d", [D, S], BF16, True)
    scr("s5_w", [NL, 4, 128, 8, 128], BF16)
    scr("s5_e", [NL, 2, 128, 8, 513], F32, True)
    scr("s5_cols", [NL, 128, 24], F32, True)
    scr("s5_dd", [NL, 128, 2, 128], BF16)
    scr("wst_bf", [NL, 128, 4, 128], BF16)
    return T


def phase_cast(nc, T, layers, tag):
    ph = Phase(nc, f"wc{tag}")
    n = 0
    for l in layers:
        jobs = [("w_in", "wbf_in", D, INC, 768), ("w_glu", "wbf_glu", 256, 256, 256), ("w_out", "wbf_out", D, D, 1024),
                ("w_up", "wbf_up", D, 2 * DFF, 512), ("w_down", "wbf_down", DFF, D, 1024)]
        for (s, d, R, C, cb) in jobs:
            src = T[s][l].rearrange("r (a b) -> (r a) b", b=cb)
            dst = T[d][l].rearrange("r (a b) -> (r a) b", b=cb)
            rows = R * (C // cb)
            step = 4096
            for r0 in range(0, rows, step):
                r1 = min(rows, r0 + step)
                ph.dma("pool", dst[r0:r1, :], src[r0:r1, :], writes=[f"cast{n}"], semkey="cast")
                n += 1
    ph.emit()


def SCAN(ph, out, d0, d1, init, reads, writes):
    ph.op("dve", lambda e: e.tensor_tensor_scan(out, d0, d1, init, ALU.mult, ALU.add), reads, writes)


def phase_s(nc, T, l):
    ph = Phase(nc, f"s{l}_")
    mask = ph.sbuf("mask", [128, 128], F32)
    iota = ph.sbuf("iota", [128, 513], F32)
    identf = ph.sbuf("identb", [128, 128], BF16)
    are = ph.sbuf("are", [128, 8], F32); aim = ph.sbuf("aim", [128, 8], F32); ldt = ph.sbuf("ldt", [128, 8], F32)
    btf_re = ph.sbuf("btf_re", [128, 8, 128], F32); btf_im = ph.sbuf("btf_im", [128, 8, 128], F32)
    ctf_re = ph.sbuf("ctf_re", [128, 8, 128], F32); ctf_im = ph.sbuf("ctf_im", [128, 8, 128], F32)
    wstf = ph.sbuf("wstf", [128, 4, 128], F32)
    dcol = ph.sbuf("dcol", [128, 2], F32)
    PR = ["params"]
    loads = [
        (mask[:], T["c_mask"]), (iota[:], T["c_iota"]), (identf[:], T["c_ident"]),
        (are[:], T["a_re_c"][l]), (aim[:], T["a_im_c"][l]), (ldt[:], T["logdt_c"][l]),
        (btf_re[:], T["bt_re"][l].rearrange("k p n -> p k n")), (btf_im[:], T["bt_im"][l].rearrange("k p n -> p k n")),
        (ctf_re[:], T["ct_re"][l].rearrange("k p n -> p k n")), (ctf_im[:], T["ct_im"][l].rearrange("k p n -> p k n")),
        (wstf[:], T["wst"][l].rearrange("h p n -> p h n")), (dcol[:], T["d_col"][l]),
    ]
    for i, (dst, src) in enumerate(loads):
        ph.dma("sp", dst, src, writes=[f"pl{i}"] if i < len(loads) - 1 else PR, semkey="params", group=True)
    dt = ph.sbuf("dt", [128, 8], F32); th = ph.sbuf("th", [128, 8], F32); ard = ph.sbuf("ard", [128, 8], F32)
    cols = ph.sbuf("cols", [128, 24], F32)
    rr = cols[:, 0:8]; r5c = cols[:, 8:16]; r5s = cols[:, 16:24]
    ec = ph.sbuf("ec", [128, 8, 513], F32); es = ph.sbuf("es", [128, 8, 513], F32)
    ACT(ph, dt[:], ldt[:], AF.Exp, PR, ["dt"])
    TT(ph, "dve", th[:], aim[:], dt[:], ALU.mult, PR + ["dt"], ["th"])
    TT(ph, "dve", ard[:], are[:], dt[:], ALU.mult, PR + ["dt"], ["ard"])
    ACT(ph, rr, ard[:], AF.Exp, ["ard"], ["rr"])
    tg = RR(ph, "tg", 2, [128, 513], F32)
    tgi = RR(ph, "tgi", 2, [128, 513], I32)
    tg2 = RR(ph, "tgb", 2, [128, 513], F32)
    for k in range(8):
        for (dst, shift, nm) in ((es, 0.0, "es"), (ec, math.pi / 2, "ec")):
            ang, angn = tg.next()
            ki, kin = tgi.next()
            red, redn = tg2.next()
            TS(ph, "dve", ang[:], iota[:], th[:, k:k + 1], shift, ALU.mult, ALU.add, PR + ["th"], [angn])
            TS(ph, "dve", ki[:], ang[:], 1.0 / TWO_PI, None, ALU.mult, None, [angn], [kin])
            STT(ph, red[:], ki[:], -CW1, ang[:], ALU.mult, ALU.add, [kin, angn], [redn])
            STT(ph, ang[:], ki[:], -CW2, red[:], ALU.mult, ALU.add, [kin, redn], [angn])
            TS(ph, "dve", red[:], ang[:], math.pi, -math.pi, ALU.min, ALU.max, [angn], [redn])
            ACT(ph, dst[:, k, :], red[:], AF.Sin, [redn], [nm])
    ph.dma("sp", T["s5_e"][l, 0], ec[:], reads=["ec"], writes=["s5e0"], semkey="st_ec")
    ph.dma("sp", T["s5_e"][l, 1], es[:], reads=["es"], writes=["s5e1"], semkey="st_es")
    c1 = ph.sbuf("c1", [128, 8], F32); s1 = ph.sbuf("s1", [128, 8], F32)
    nre = ph.sbuf("nre", [128, 8], F32); nim = ph.sbuf("nim", [128, 8], F32)
    den = ph.sbuf("den", [128, 8], F32); tmpc = ph.sbuf("tmpc", [128, 8], F32); tmpd = ph.sbuf("tmpd", [128, 8], F32)
    cre = ph.sbuf("cre", [128, 8], F32); cim = ph.sbuf("cim", [128, 8], F32); ncim = ph.sbuf("ncim", [128, 8], F32)
    CP(ph, "dve", c1[:], ec[:, :, 1], ["ec"], ["c1"])
    CP(ph, "dve", s1[:], es[:, :, 1], ["es"], ["s1"])
    CP(ph, "dve", r5c, ec[:, :, 512], ["ec"], ["r5c"])
    CP(ph, "dve", r5s, es[:, :, 512], ["es"], ["r5s"])
    ph.dma("sp", T["s5_cols"][l], cols[:], reads=["rr", "r5c", "r5s"], writes=["s5cols"], semkey="st_cols")
    TT(ph, "dve", nre[:], rr, c1[:], ALU.mult, ["rr", "c1"], ["nre"])
    TS(ph, "dve", nre[:], nre[:], -1.0, None, ALU.add, None, ["nre"], ["nre"])
    TT(ph, "dve", nim[:], rr, s1[:], ALU.mult, ["rr", "s1"], ["nim"])
    TT(ph, "dve", den[:], are[:], are[:], ALU.mult, PR, ["den"])
    TT(ph, "dve", tmpc[:], aim[:], aim[:], ALU.mult, PR, ["tmpc"])
    TT(ph, "dve", den[:], den[:], tmpc[:], ALU.add, ["den", "tmpc"], ["den"])
    RECIP(ph, den[:], den[:], ["den"], ["den"])
    TT(ph, "dve", tmpc[:], nre[:], are[:], ALU.mult, ["nre"] + PR, ["tmpc"])
    TT(ph, "dve", tmpd[:], nim[:], aim[:], ALU.mult, ["nim"] + PR, ["tmpd"])
    TT(ph, "dve", tmpc[:], tmpc[:], tmpd[:], ALU.add, ["tmpc", "tmpd"], ["tmpc"])
    TT(ph, "dve", cre[:], tmpc[:], den[:], ALU.mult, ["tmpc", "den"], ["cre"])
    TT(ph, "dve", tmpc[:], nim[:], are[:], ALU.mult, ["nim"] + PR, ["tmpc"])
    TT(ph, "dve", tmpd[:], nre[:], aim[:], ALU.mult, ["nre"] + PR, ["tmpd"])
    TT(ph, "dve", tmpc[:], tmpc[:], tmpd[:], ALU.subtract, ["tmpc", "tmpd"], ["tmpc"])
    TT(ph, "dve", cim[:], tmpc[:], den[:], ALU.mult, ["tmpc", "den"], ["cim"])
    TS(ph, "dve", ncim[:], cim[:], -1.0, None, ALU.mult, None, ["cim"], ["ncim"])
    w4 = ph.sbuf("w4", [128, 4, 8, 128], BF16)
    ctmp = RR(ph, "ctmp", 2, [128, 128], F32)
    CP(ph, "dve", w4[:, 0], btf_re[:], PR, ["w4_0"])
    CP(ph, "dve", w4[:, 1], btf_im[:], PR, ["w4_1"])
    for k in range(8):
        t1, t1n = ctmp.next()
        TS(ph, "dve", t1[:], ctf_im[:, k, :], ncim[:, k:k + 1], None, ALU.mult, None, PR + ["ncim"], [t1n])
        STT(ph, w4[:, 2, k, :], ctf_re[:, k, :], cre[:, k:k + 1], t1[:], ALU.mult, ALU.add, PR + ["cre", t1n], ["w4_2"])
        t2, t2n = ctmp.next()
        TS(ph, "dve", t2[:], ctf_im[:, k, :], cre[:, k:k + 1], None, ALU.mult, None, PR + ["cre"], [t2n])
        STT(ph, t2[:], ctf_re[:, k, :], cim[:, k:k + 1], t2[:], ALU.mult, ALU.add, PR + ["cim", t2n], [t2n])
        TS(ph, "dve", w4[:, 3, k, :], t2[:], -1.0, None, ALU.mult, None, [t2n], ["w4_3"])
    ph.dma("sp", T["s5_w"][l].rearrange("f p k n -> p f k n"), w4[:], reads=["w4_0", "w4_1", "w4_2", "w4_3"], writes=["s5w"],
           semkey="st_w4")
    wst = ph.sbuf("wst", [128, 4, 128], BF16)
    for hh in range(4):
        TT(ph, "dve", wst[:, hh, :], wstf[:, hh, :], mask[:], ALU.mult, PR, ["wst"])
    ph.dma("sp", T["wst_bf"][l], wst[:], reads=["wst"], writes=["wstd"], semkey="st_wst")
    dd = ph.sbuf("dd", [128, 2, 128], BF16)
    for m in range(2):
        TS(ph, "dve", dd[:, m, :], identf[:], dcol[:, m:m + 1], None, ALU.mult, None, PR, ["dd"])
    ph.dma("sp", T["s5_dd"][l], dd[:], reads=["dd"], writes=["ddd"], semkey="st_dd")
    ph.emit()


def phase_a(nc, T, l, hsrc):
    ph = Phase(nc, f"a{l}_")
    w_in = ph.sbuf("w_in", [128, 8, INC], BF16)
    g1b = ph.sbuf("g1b", [128, D], F32)
    ident = ph.sbuf("ident", [128, 128], BF16)
    rm = ph.sbuf("rm", [128, 128], BF16)
    mblk = ph.sbuf("mblk", [128, 128], BF16)
    bglu = ph.sbuf("bglu", [128, 2], F32)
    vgain = ph.sbuf("vgain", [128, 2], F32)
    bsb = ph.sbuf("bsb", [128, 2, 128], F32)
    qkc = ph.sbuf("qkc", [128, 4], F32)
    wglu = ph.sbuf("wglu", [128, 2, 256], BF16)
    w4 = ph.sbuf("w4", [128, 4, 8, 128], BF16)
    ecs = ph.sbuf("ecs", [128, 2, 8, 513], F32)
    cols = ph.sbuf("cols", [128, 24], F32)
    dd = ph.sbuf("dd", [128, 2, 128], BF16)
    wst = ph.sbuf("wst", [128, 4, 128], BF16)
    PR = ["params"]
    loads = [
        (w_in[:], T["wbf_in"][l].rearrange("(kt p) n -> p kt n", p=128)),
        (g1b[:], bcast_rows(T["g1"][l:l + 1, :], 128, D)),
        (ident[:], T["c_ident"]), (rm[:], T["c_rm"]), (mblk[:], T["c_mblk"]),
        (bglu[:], T["bglu_col"][l]), (vgain[:], T["vgain_col"][l]),
        (bsb[:], T["bs_b"][l].rearrange("h p n -> p h n")),
        (qkc[:], T["qk_cols"][l]),
        (wglu[:], T["wbf_glu"][l].rearrange("(mi p) n -> p mi n", p=128)),
        (w4[:], T["s5_w"][l].rearrange("f p k n -> p f k n")),
        (ecs[:, 0], T["s5_e"][l, 0]), (ecs[:, 1], T["s5_e"][l, 1]),
        (cols[:], T["s5_cols"][l]), (dd[:], T["s5_dd"][l]), (wst[:], T["wst_bf"][l]),
    ]
    for i, (dst, src) in enumerate(loads):
        ph.dma("sp", dst, src, writes=[f"pl{i}"] if i < len(loads) - 1 else PR, semkey="params", group=True)
    ec = ecs[:, 0]
    es = ecs[:, 1]
    rr = cols[:, 0:8]; r5c = cols[:, 8:16]; r5s = cols[:, 16:24]
    bt_re = w4[:, 0]; bt_im = w4[:, 1]; cp_re = w4[:, 2]; cp_nim = w4[:, 3]

    hp = RR(ph, "hin", 2, [128, D], F32)
    junk = ph.sbuf("junk", [128, D], BF16)
    ssq = RR(ph, "ssq", 4, [128, 1], F32)
    rst = RR(ph, "rst", 4, [128, 1], F32)
    xnp = RR(ph, "xn", 2, [128, D], BF16)
    tpp = RR(ph, "tp", 2, [128, 8, 128], BF16, psum=True)
    xnT = RR(ph, "xnT", 1, [128, 8, 512], BF16)
    mmp = RR(ph, "mm", 3, [128, 512], F32, psum=True)
    aux = RR(ph, "aux", 3, [128, 512], F32, psum=True)
    u_bf = ph.sbuf("u_bf", [128, 2, 512], BF16)
    ug_f = ph.sbuf("ug_f", [128, 2, 512], F32)
    cosp = RR(ph, "cos", 1, [128, 512], F32)
    sinp = RR(ph, "sin", 1, [128, 512], F32)
    sqp = RR(ph, "sq", 2, [128, 512], BF16)
    qbp = RR(ph, "qb", 2, [128, 512], BF16)
    f32p = RR(ph, "f32t", 4, [128, 512], F32)
    qkout = RR(ph, "qko", 2, [128, 512], BF16)
    vhat = RR(ph, "vhat", 2, [128, 256], BF16)
    vbf = RR(ph, "vbf", 2, [128, 512], BF16)
    sgt = RR(ph, "sgt", 2, [128, 2, 128], F32)
    mixA = RR(ph, "mixA", 1, [128, 4, 512], BF16)
    mp = RR(ph, "s5m", 4, [128, 512], F32)
    wp = RR(ph, "s5w", 2, [128, 512], F32)
    zp = RR(ph, "s5z", 2, [128, 512], F32)
    xp = RR(ph, "s5x", 8, [128, 512], BF16)
    zlast = ph.sbuf("zlast", [128, 8, 2], F32)
    zinit = ph.sbuf("zinit", [128, 8, 2], F32)
    ztmp = RR(ph, "ztmp", 4, [128, 1], F32)
    yg = ph.sbuf("yg", [128, 2, 512], F32)
    ygb = ph.sbuf("ygb", [128, 2, 512], BF16)
    sig = RR(ph, "sig", 1, [128, 512], F32)

    W_US, W_UG, W_VG, W_Q, W_K, W_V = 0, 256, 512, 768, 1280, 1792

    for b in range(NBLK):
        t0 = 512 * b
        xT, xTn = xnT.next()
        xT_parts = [f"{xTn}_{j}" for j in range(4)]
        mA, mAn = mixA.next()
        for j in range(4):
            t = 4 * b + j
            ht, htn = hp.next()
            ph.dma("sp", ht[:], hsrc[128 * t:128 * t + 128, :], writes=[htn], semkey=("ld", htn))
            ss, ssn = ssq.next()
            rs, rsn = rst.next()
            ACT(ph, junk[:], ht[:], AF.Square, [htn], ["junk", ssn], accum_out=ss[:])
            ACT(ph, rs[:], ss[:], AF.Sqrt, [ssn], [rsn], scale=1.0 / D, bias=EPS)
            RECIP(ph, rs[:], rs[:], [rsn], [rsn])
            xn, xnn = xnp.next()
            STT(ph, xn[:], ht[:], rs[:], g1b[:], ALU.mult, ALU.mult, [htn, rsn] + PR, [xnn])
            tp, tpn = tpp.next()
            for kt in range(8):
                TR(ph, tp[:, kt, :], xn[:, 128 * kt:128 * kt + 128], ident[:], [xnn] + PR, [tpn])
            CP(ph, "act", xT[:, :, 128 * j:128 * j + 128], tp[:], [tpn], [xT_parts[j]])
        cosb, cosn = cosp.next()
        sinb, sinn = sinp.next()
        ph.dma("sp", cosb[:], T["c_cos"][:, t0:t0 + 512], writes=[cosn], semkey=("ld", cosn))
        ph.dma("sp", sinb[:], T["c_sin"][:, t0:t0 + 512], writes=[sinn], semkey=("ld", sinn))

        def inproj_fm(c0):
            ps, psn = mmp.next()
            for kt in range(8):
                MM(ph, ps[:], w_in[:, kt, c0:c0 + 128], xT[:, kt, :], kt == 0, kt == 7, xT_parts + PR, [psn])
            return ps, psn

        for m in range(2):
            ps, psn = inproj_fm(W_US + 128 * m)
            CP(ph, "act", u_bf[:, m, :], ps[:], [psn], [f"u_bf{m}"])
        for m in range(2):
            ps, psn = inproj_fm(W_UG + 128 * m)
            CP(ph, "act", ug_f[:, m, :], ps[:], [psn], [f"ug_f{m}"])

        xs = {}
        for k in range(8):
            m = k // 4
            raw_re, rren = aux.next()
            raw_im, rimn = aux.next()
            MM(ph, raw_re[:], bt_re[:, k, :], u_bf[:, m, :], True, True, PR + [f"u_bf{m}"], [rren])
            MM(ph, raw_im[:], bt_im[:, k, :], u_bf[:, m, :], True, True, PR + [f"u_bf{m}"], [rimn])
            m1, m1n = mp.next(); m2, m2n = mp.next(); m3, m3n = mp.next(); m4, m4n = mp.next()
            TT(ph, "dve", m1[:], raw_re[:], ec[:, k, 0:512], ALU.mult, [rren] + PR, [m1n])
            TT(ph, "dve", m2[:], raw_im[:], es[:, k, 0:512], ALU.mult, [rimn] + PR, [m2n])
            TT(ph, "dve", m3[:], raw_im[:], ec[:, k, 0:512], ALU.mult, [rimn] + PR, [m3n])
            TT(ph, "dve", m4[:], raw_re[:], es[:, k, 0:512], ALU.mult, [rren] + PR, [m4n])
            w_re, wren = wp.next(); w_im, wimn = wp.next()
            TT(ph, "pool", w_re[:], m1[:], m2[:], ALU.add, [m1n, m2n], [wren])
            TT(ph, "pool", w_im[:], m3[:], m4[:], ALU.subtract, [m3n, m4n], [wimn])
            z_re, zren = zp.next(); z_im, zimn = zp.next()
            rb = col_bcast(rr[:, k:k + 1], 512)
            if b == 0:
                SCAN(ph, z_re[:], rb, w_re[:], 0.0, PR + [wren], [zren])
                SCAN(ph, z_im[:], rb, w_im[:], 0.0, PR + [wimn], [zimn])
            else:
                ta, tan = ztmp.next(); tb, tbn = ztmp.next()
                zl = f"zlast{k}"
                TT(ph, "dve", ta[:], zlast[:, k, 1:2], r5s[:, k:k + 1], ALU.mult, [zl] + PR, [tan])
                STT(ph, zinit[:, k, 0:1], zlast[:, k, 0:1], r5c[:, k:k + 1], ta[:], ALU.mult, ALU.subtract, [zl, tan] + PR, [f"zinit{k}"])
                TT(ph, "dve", tb[:], zlast[:, k, 1:2], r5c[:, k:k + 1], ALU.mult, [zl] + PR, [tbn])
                STT(ph, zinit[:, k, 1:2], zlast[:, k, 0:1], r5s[:, k:k + 1], tb[:], ALU.mult, ALU.add, [zl, tbn, f"zinit{k}"] + PR, [f"zinit{k}"])
                SCAN(ph, z_re[:], rb, w_re[:], zinit[:, k, 0:1], PR + [wren, f"zinit{k}"], [zren])
                SCAN(ph, z_im[:], rb, w_im[:], zinit[:, k, 1:2], PR + [wimn, f"zinit{k}"], [zimn])
            CP(ph, "pool", zlast[:, k, 0:1], z_re[:, 511:512], [zren], [f"zlast{k}"])
            CP(ph, "pool", zlast[:, k, 1:2], z_im[:, 511:512], [zimn, f"zlast{k}"], [f"zlast{k}"])
            m1, m1n = mp.next(); m2, m2n = mp.next(); m3, m3n = mp.next(); m4, m4n = mp.next()
            TT(ph, "pool", m1[:], z_re[:], ec[:, k, 0:512], ALU.mult, [zren] + PR, [m1n])
            TT(ph, "pool", m2[:], z_im[:], es[:, k, 0:512], ALU.mult, [zimn] + PR, [m2n])
            TT(ph, "pool", m3[:], z_im[:], ec[:, k, 0:512], ALU.mult, [zimn] + PR, [m3n])
            TT(ph, "pool", m4[:], z_re[:], es[:, k, 0:512], ALU.mult, [zren] + PR, [m4n])
            x_re, xren = xp.next(); x_im, ximn = xp.next()
            TT(ph, "pool", x_re[:], m1[:], m2[:], ALU.subtract, [m1n, m2n], [xren])
            TT(ph, "pool", x_im[:], m3[:], m4[:], ALU.add, [m3n, m4n], [ximn])
            xs[k] = (x_re, xren, x_im, ximn)
            if k % 4 == 3:
                yps, ypsn = aux.next()
                first = True
                for kk in range(k - 3, k + 1):
                    xr, xrn, xi, xin = xs[kk]
                    MM(ph, yps[:], cp_re[:, kk, :], xr[:], first, False, PR + [xrn], [ypsn])
                    MM(ph, yps[:], cp_nim[:, kk, :], xi[:], False, False, PR + [xin], [ypsn])
                    first = False
                MM(ph, yps[:], dd[:, m, :], u_bf[:, m, :], False, True, PR + [f"u_bf{m}"], [ypsn])
                ACT(ph, yg[:, m, :], yps[:], AF.Gelu_apprx_tanh, [ypsn], [f"yg{m}"])
                CP(ph, "pool", ygb[:, m, :], yg[:, m, :], [f"yg{m}"], [f"ygb{m}"])
        for mo in range(2):
            gps, gpsn = aux.next()
            for mi in range(2):
                MM(ph, gps[:], wglu[:, mi, 128 * mo:128 * mo + 128], ygb[:, mi, :], mi == 0, mi == 1, PR + ["ygb0", "ygb1"], [gpsn])
            sg, sgn = sig.next()
            ACT(ph, sg[:], gps[:], AF.Sigmoid, [gpsn] + PR, [sgn], bias=bglu[:, mo:mo + 1])
            TT(ph, "pool", mA[:, mo, :], yg[:, mo, :], sg[:], ALU.mult, [f"yg{mo}", sgn], [f"{mAn}_{mo}"])

        for which, wbase, dst, gi in (("q", W_Q, "qT_d", 0), ("k", W_K, "kT_d", 2)):
            for hd in range(4):
                ps, psn = inproj_fm(wbase + 128 * hd)
                sq, sqn = sqp.next()
                qb, qbn = qbp.next()
                ACT(ph, sq[:], ps[:], AF.Square, [psn], [sqn])
                ACT(ph, qb[:], ps[:], AF.Copy, [psn], [qbn])
                ms, msn = aux.next()
                rq, rqn = aux.next()
                MM(ph, ms[:], mblk[:], sq[:], True, True, PR + [sqn], [msn])
                MM(ph, rq[:], rm[:], qb[:], True, True, PR + [qbn], [rqn])
                rsd, rsdn = f32p.next()
                ACT(ph, rsd[:], ms[:], AF.Sqrt, [msn], [rsdn], bias=EPS)
                RECIP(ph, rsd[:], rsd[:], [rsdn], [rsdn])
                t1, t1n = f32p.next()
                t2, t2n = f32p.next()
                STT(ph, t1[:], ps[:], qkc[:, gi:gi + 1], cosb[:], ALU.mult, ALU.mult, [psn, cosn] + PR, [t1n])
                STT(ph, t2[:], rq[:], qkc[:, gi + 1:gi + 2], sinb[:], ALU.mult, ALU.mult, [rqn, sinn] + PR, [t2n])
                TT(ph, "pool", t1[:], t1[:], t2[:], ALU.add, [t1n, t2n], [t1n])
                qo, qon = qkout.next()
                TT(ph, "pool", qo[:], t1[:], rsd[:], ALU.mult, [t1n, rsdn], [qon])
                ph.dma("sp", T[dst][hd, :, t0:t0 + 512], qo[:], reads=[qon], writes=[f"{dst}{hd}_{b}"], semkey=("st", qon))

        for j in range(4):
            t = 4 * b + j
            ps, psn = mmp.next()
            for kt in range(8):
                MM(ph, ps[:, 0:256], xT[:, kt, 128 * j:128 * j + 128], w_in[:, kt, W_VG:W_VG + 256], kt == 0, kt == 7,
                   xT_parts + PR, [psn])
            ss, ssn = ssq.next()
            rs, rsn = rst.next()
            ACT(ph, junk[:, 0:256], ps[:, 0:256], AF.Square, [psn], ["junk", ssn], accum_out=ss[:])
            ACT(ph, rs[:], ss[:], AF.Sqrt, [ssn], [rsn], scale=1.0 / 256, bias=EPS)
            RECIP(ph, rs[:], rs[:], [rsn], [rsn])
            vh, vhn = vhat.next()
            ACT(ph, vh[:], ps[:, 0:256], AF.Copy, [psn, rsn], [vhn], scale=rs[:])
            sgp, sgpn = aux.next()
            for hh in range(4):
                pr, lo = hh // 2, 64 * (hh % 2)
                MM(ph, sgp[lo:lo + 64, 128 * pr:128 * pr + 128], vh[:, 64 * hh:64 * hh + 64], wst[:, hh, :], True, True,
                   [vhn] + PR, [sgpn])
            st_, stn = sgt.next()
            for pr in range(2):
                STT(ph, st_[:, pr, :], sgp[:, 128 * pr:128 * pr + 128], vgain[:, pr:pr + 1], bsb[:, pr, :], ALU.mult, ALU.add,
                    [sgpn] + PR, [f"{stn}_{pr}"])
                TT(ph, "pool", mA[:, 2 + pr, 128 * j:128 * j + 128], st_[:, pr, :], ug_f[:, pr, 128 * j:128 * j + 128], ALU.mult,
                   [f"{stn}_{pr}", f"ug_f{pr}"], [f"{mAn}_sg{pr}_{j}"])
            ps, psn = mmp.next()
            for kt in range(8):
                MM(ph, ps[:], xT[:, kt, 128 * j:128 * j + 128], w_in[:, kt, W_V:W_V + 512], kt == 0, kt == 7, xT_parts + PR, [psn])
            vb, vbn = vbf.next()
            CP(ph, "act", vb[:], ps[:], [psn], [vbn])
            ph.dma("sp", T["V_d"][128 * t:128 * t + 128, :], vb[:], reads=[vbn], writes=[f"V_d{t}"], semkey=("st", vbn))
        mA_res = [f"{mAn}_{mo}" for mo in range(2)] + [f"{mAn}_sg{pr}_{j}" for pr in range(2) for j in range(4)]
        ph.dma("sp", T["mixT_d"].rearrange("(m p) t -> p m t", p=128)[:, 0:4, t0:t0 + 512], mA[:], reads=mA_res,
               writes=[f"mixA_d{b}"], semkey=("st", mAn))
    ph.emit()


def phase_b(nc, T, l):
    lambda_init = 0.8 - 0.6 * math.exp(-0.3 * l)
    ph = Phase(nc, f"b{l}_")
    ones = ph.sbuf("ones", [128, 128], BF16)
    maskb = ph.sbuf("maskb", [128, 128], BF16)
    lamr = ph.sbuf("lamr", [128, 256], F32)
    subc = ph.sbuf("subc", [128, 1], F32)
    PR = ["params"]
    loads = [(ones[:], T["c_ones"]), (maskb[:], T["c_maskb"]),
             (lamr[:], bass.AP(T["lam_rows"].tensor, T["lam_rows"][l].offset, [[0, 128], [1, 256]])),
             (subc[:], T["subln_col"][l])]
    for i, (dst, src) in enumerate(loads):
        ph.dma("sp", dst, src, writes=[f"pl{i}"] if i < len(loads) - 1 else PR, semkey="params", group=True)
    lprod = ph.sbuf("lprod", [128, 2, 64], F32)
    ljunk = ph.sbuf("ljunk", [128, 64], F32)
    lsum = ph.sbuf("lsum", [128, 2], F32)
    lexp = ph.sbuf("lexp", [128, 2], F32)
    neglam = ph.sbuf("neglam", [128, 1], F32)
    subs = ph.sbuf("subs", [128, 1], F32)
    TT(ph, "dve", lprod[:, 0, :], lamr[:, 0:64], lamr[:, 64:128], ALU.mult, PR, ["lprod0"])
    TT(ph, "dve", lprod[:, 1, :], lamr[:, 128:192], lamr[:, 192:256], ALU.mult, PR, ["lprod1"])
    ACT(ph, ljunk[:], lprod[:, 0, :], AF.Copy, ["lprod0"], ["ljunk", "lsum0"], accum_out=lsum[:, 0:1])
    ACT(ph, ljunk[:], lprod[:, 1, :], AF.Copy, ["lprod1"], ["ljunk", "lsum1"], accum_out=lsum[:, 1:2])
    ACT(ph, lexp[:], lsum[:], AF.Exp, ["lsum0", "lsum1"], ["lexp"])
    TT(ph, "dve", neglam[:], lexp[:, 1:2], lexp[:, 0:1], ALU.subtract, ["lexp"], ["neglam"])
    TS(ph, "dve", neglam[:], neglam[:], -lambda_init, None, ALU.add, None, ["neglam"], ["neglam"])
    TS(ph, "dve", subs[:], subc[:], 1.0 - lambda_init, None, ALU.mult, None, PR, ["subs"])

    kTp = RR(ph, "kT", 2, [128, S], BF16)
    Vp = RR(ph, "V", 2, [128, NT, 128], BF16)
    qp = RR(ph, "q", 2, [128, 512], BF16)
    psS = RR(ph, "psS", 3, [128, 512], F32, psum=True)
    acc = RR(ph, "acc", 4, [128, 512], F32, psum=True)
    pTp = RR(ph, "pT", 4, [128, 512], BF16)
    rlp = RR(ph, "rl", 2, [128, 512], F32)
    op_ = RR(ph, "o", 4, [128, 512], F32)
    sqp = RR(ph, "sq", 2, [128, 512], BF16)
    rsp = RR(ph, "rs", 2, [128, 512], F32)
    outp = RR(ph, "ob", 2, [128, 512], BF16)

    for hd in range(4):
        kT, kTn = kTp.next()
        V, Vn = Vp.next()
        ph.dma("sp", kT[:], T["kT_d"][hd], writes=[kTn], semkey=("ld", kTn))
        ph.dma("sp", V[:], T["V_d"].rearrange("(t p) e -> p t e", p=128)[:, :, 128 * hd:128 * hd + 128], writes=[Vn],
               semkey=("ld", Vn))
        for qb in range(NBLK):
            q, qn = qp.next()
            ph.dma("sp", q[:], T["qT_d"][hd, :, 512 * qb:512 * qb + 512], writes=[qn], semkey=("ld", qn))
            os_ = []
            for c in range(2):
                pv, pvn = acc.next()
                ls, lsn = acc.next()
                nkt = 4 * qb + 4
                for kt in range(nkt):
                    m = kt - 4 * qb
                    c0 = 128 * m if m > 0 else 0
                    st, stn = psS.next()
                    MM(ph, st[:, c0:512], kT[64 * c:64 * c + 64, 128 * kt:128 * kt + 128], q[64 * c:64 * c + 64, c0:512], True, True,
                       [kTn, qn], [stn])
                    pT, pTn = pTp.next()
                    ACT(ph, pT[:, c0:512], st[:, c0:512], AF.Exp, [stn], [pTn], scale=0.125)
                    if m >= 0:
                        TT(ph, "pool", pT[:, c0:c0 + 128], pT[:, c0:c0 + 128], maskb[:], ALU.mult, [pTn] + PR, [pTn])
                    MM(ph, pv[:, c0:512], V[:, kt, :], pT[:, c0:512], kt == 0, kt == nkt - 1, [Vn, pTn], [pvn])
                    MM(ph, ls[:, c0:512], ones[:], pT[:, c0:512], kt == 0, kt == nkt - 1, PR + [pTn], [lsn])
                rl, rln = rlp.next()
                RECIP(ph, rl[:], ls[:], [lsn], [rln])
                o, on = op_.next()
                TT(ph, "dve", o[:], pv[:], rl[:], ALU.mult, [pvn, rln], [on])
                os_.append((o, on))
            (o0, o0n), (o1, o1n) = os_
            STT(ph, o0[:], o1[:], neglam[:], o0[:], ALU.mult, ALU.add, [o0n, o1n, "neglam"], [o0n])
            sq, sqn = sqp.next()
            ACT(ph, sq[:], o0[:], AF.Square, [o0n], [sqn])
            ms, msn = psS.next()
            MM(ph, ms[:], ones[:], sq[:], True, True, PR + [sqn], [msn])
            rs, rsn = rsp.next()
            ACT(ph, rs[:], ms[:], AF.Sqrt, [msn], [rsn], scale=1.0 / 128, bias=EPS)
            RECIP(ph, rs[:], rs[:], [rsn], [rsn])
            ob, obn = outp.next()
            STT(ph, ob[:], o0[:], subs[:], rs[:], ALU.mult, ALU.mult, [o0n, "subs", rsn], [obn])
            ph.dma("sp", T["mixT_d"][512 + 128 * hd:512 + 128 * hd + 128, 512 * qb:512 * qb + 512], ob[:], reads=[obn],
                   writes=[f"mixB{hd}_{qb}"], semkey=("st", obn))
    ph.emit()


def phase_c(nc, T, l, hsrc, hdst):
    ph = Phase(nc, f"c{l}_")
    w_out = ph.sbuf("w_out", [128, 8, D], BF16)
    g2b = ph.sbuf("g2b", [128, D], F32)
    ident = ph.sbuf("ident", [128, 128], BF16)
    cw = ph.sbuf("cw", [128, 44, 3], F32)
    cb = ph.sbuf("cb", [128, 44], F32)
    PR = ["params"]
    loads = [(w_out[:], T["wbf_out"][l].rearrange("(kt p) n -> p kt n", p=128)),
             (g2b[:], bcast_rows(T["g2"][l:l + 1, :], 128, D)), (ident[:], T["c_ident"]),
             (cw[:], T["cw_col"][l]), (cb[:], T["cb_col"][l])]
    for i, (dst, src) in enumerate(loads):
        ph.dma("sp", dst, src, writes=[f"pl{i}"] if i < len(loads) - 1 else PR, semkey="params", group=True)
    halo = ph.sbuf("halo", [128, 44, 2], F32)
    ph.op("pool", lambda e: e.memset(halo[:], 0.0), [], ["halo"])

    mixp = RR(ph, "mix", 2, [128, 8, 512], BF16)
    hp = RR(ph, "hin", 3, [128, D], F32)
    h2 = ph.sbuf("h2", [128, 4, D], F32)
    junk = ph.sbuf("junk", [128, D], BF16)
    ssq = RR(ph, "ssq", 4, [128, 1], F32)
    rst = RR(ph, "rst", 4, [128, 1], F32)
    xnp = RR(ph, "xn", 2, [128, D], BF16)
    tpp = RR(ph, "tp", 1, [128, 8, 128], BF16, psum=True)
    xnT = RR(ph, "xnT", 2, [128, 8, 512], BF16)
    mmp = RR(ph, "mm", 3, [128, 512], F32, psum=True)
    accp = RR(ph, "dacc", 4, [128, 512], F32, psum=True)
    wupp = RR(ph, "wup", 3, [128, 8, 256], BF16)
    wdnp = RR(ph, "wdn", 4, [128, 512], BF16)
    hbp = RR(ph, "hb", 4, [128, 514], F32)
    accs = RR(ph, "cacc", 4, [128, 512], F32)
    gl = RR(ph, "gl", 2, [128, 512], F32)
    prodT = ph.sbuf("prodT", [128, NFT, 512], BF16)
    outt = RR(ph, "outt", 3, [128, 512], F32)
    wup_d = T["wbf_up"][l].rearrange("(kt p) n -> p kt n", p=128)

    for b in range(NBLK):
        t0 = 512 * b
        mix, mixn = mixp.next()
        ph.dma("sp", mix[:], T["mixT_d"].rearrange("(m p) t -> p m t", p=128)[:, :, t0:t0 + 512], writes=[mixn], semkey=("ld", mixn))
        xT, xTn = xnT.next()
        xT_parts = [f"{xTn}_{j}" for j in range(4)]
        for j in range(4):
            t = 4 * b + j
            ht, htn = hp.next()
            ph.dma("sp", ht[:], hsrc[128 * t:128 * t + 128, :], writes=[htn], semkey=("ld", htn))
            for half in range(2):
                ps, psn = mmp.next()
                for kt in range(8):
                    MM(ph, ps[:], mix[:, kt, 128 * j:128 * j + 128], w_out[:, kt, 512 * half:512 * half + 512], kt == 0, kt == 7,
                       [mixn] + PR, [psn])
                TT(ph, "dve", h2[:, j, 512 * half:512 * half + 512], ps[:], ht[:, 512 * half:512 * half + 512], ALU.add, [psn, htn],
                   [f"h2_{j}_{half}"])
            h2r = [f"h2_{j}_0", f"h2_{j}_1"]
            ss, ssn = ssq.next()
            rs, rsn = rst.next()
            ACT(ph, junk[:], h2[:, j, :], AF.Square, h2r, ["junk", ssn], accum_out=ss[:])
            ACT(ph, rs[:], ss[:], AF.Sqrt, [ssn], [rsn], scale=1.0 / D, bias=EPS)
            RECIP(ph, rs[:], rs[:], [rsn], [rsn])
            xn, xnn = xnp.next()
            STT(ph, xn[:], h2[:, j, :], rs[:], g2b[:], ALU.mult, ALU.mult, h2r + [rsn] + PR, [xnn])
            tp, tpn = tpp.next()
            for kt in range(8):
                TR(ph, tp[:, kt, :], xn[:, 128 * kt:128 * kt + 128], ident[:], [xnn] + PR, [tpn])
            CP(ph, "act", xT[:, :, 128 * j:128 * j + 128], tp[:], [tpn], [xT_parts[j]])
        for ft in range(NFT):
            wu, wun = wupp.next()
            ph.dma("sp", wu[:, :, 0:128], wup_d[:, :, 128 * ft:128 * ft + 128], writes=[wun + "g"], semkey=("ld", wun + "g"))
            ph.dma("sp", wu[:, :, 128:256], wup_d[:, :, DFF + 128 * ft:DFF + 128 * ft + 128], writes=[wun + "v"],
                   semkey=("ld", wun + "v"))
            res = []
            for gv in range(2):
                fi = ft + NFT * gv
                ps, psn = mmp.next()
                for kt in range(8):
                    MM(ph, ps[:], wu[:, kt, 128 * gv:128 * gv + 128], xT[:, kt, :], kt == 0, kt == 7,
                       [wun + ("g" if gv == 0 else "v")] + xT_parts, [psn])
                hb, hbn = hbp.next()
                CP(ph, "pool", hb[:, 0:2], halo[:, fi, :], [f"halo{fi}", "halo"], [hbn + "h"])
                CP(ph, "act", hb[:, 2:514], ps[:], [psn], [hbn])
                CP(ph, "pool", halo[:, fi, :], hb[:, 512:514], [hbn, hbn + "h"], [f"halo{fi}"])
                ca, can = accs.next()
                ACT(ph, ca[:], ps[:], AF.Identity, [psn] + PR, [can], scale=cw[:, fi, 2:3], bias=cb[:, fi:fi + 1])
                STT(ph, ca[:], hb[:, 1:513], cw[:, fi, 1:2], ca[:], ALU.mult, ALU.add, [hbn, hbn + "h", can] + PR, [can])
                STT(ph, ca[:], hb[:, 0:512], cw[:, fi, 0:1], ca[:], ALU.mult, ALU.add, [hbn, hbn + "h", can] + PR, [can])
                res.append((ca, can))
            (cg, cgn), (cv, cvn) = res
            g, gn = gl.next()
            ACT(ph, g[:], cg[:], AF.Gelu_apprx_tanh, [cgn], [gn])
            TT(ph, "pool", prodT[:, ft, :], g[:], cv[:], ALU.mult, [gn, cvn], [f"prod{ft}"])
        for half in range(2):
            accl = [accp.next() for _ in range(4)]
            for ft in range(NFT):
                wd, wdn = wdnp.next()
                ph.dma("sp", wd[:], T["wbf_down"][l, 128 * ft:128 * ft + 128, 512 * half:512 * half + 512], writes=[wdn],
                       semkey=("ld", wdn))
                for j in range(4):
                    a, an = accl[j]
                    MM(ph, a[:], prodT[:, ft, 128 * j:128 * j + 128], wd[:], ft == 0, ft == NFT - 1, [f"prod{ft}", wdn], [an])
            for j in range(4):
                t = 4 * b + j
                a, an = accl[j]
                ot, otn = outt.next()
                TT(ph, "dve", ot[:], a[:], h2[:, j, 512 * half:512 * half + 512], ALU.add, [an, f"h2_{j}_{half}"], [otn])
                ph.dma("sp", hdst[128 * t:128 * t + 128, 512 * half:512 * half + 512], ot[:], reads=[otn],
                       writes=[f"hd{t}_{half}"], semkey=("st", otn))
    ph.emit()


def build(nlayers=NL, debug=False, phases="abc"):
    nc = bass.Bass("TRN2", target_bir_lowering=False)
    T = declare(nc, debug)
    phase_cast(nc, T, list(range(nlayers)), 0)
    for l in range(nlayers):
        phase_s(nc, T, l)
    for l in range(nlayers):
        hsrc = T["x"] if l == 0 else T["out"]
        if "a" in phases:
            phase_a(nc, T, l, hsrc)
        if "b" in phases:
            phase_b(nc, T, l)
        if "c" in phases:
            phase_c(nc, T, l, hsrc, T["out"])
    return nc


def _bf(a):
    return np.ascontiguousarray(a).astype(ml_dtypes.bfloat16)


def prep_shared(inp):
    f = lambda k: np.ascontiguousarray(np.asarray(inp[k], dtype=np.float32))
    sh = {}
    sh["w_in"] = f("w_in"); sh["w_out"] = f("w_out"); sh["w_up"] = f("w_up"); sh["w_down"] = f("w_down")
    sh["w_glu"] = f("ssm_w_glu")
    sh["g1"] = f("attn_norm_g"); sh["g2"] = f("ffn_norm_g")

    def state_cols(a):
        return np.ascontiguousarray(a.reshape(NL, 8, 128).transpose(0, 2, 1))
    sh["a_re_c"] = state_cols(f("ssm_a_re")); sh["a_im_c"] = state_cols(f("ssm_a_im"))
    sh["logdt_c"] = state_cols(np.repeat(f("ssm_log_dt")[:, :, None], 64, axis=2))
    b_re, b_im, c_re, c_im = f("ssm_b_re"), f("ssm_b_im"), f("ssm_c_re"), f("ssm_c_im")
    bt_re = np.zeros((NL, 8, 128, 128), np.float32); bt_im = np.zeros_like(bt_re)
    ct_re = np.zeros_like(bt_re); ct_im = np.zeros_like(bt_re)
    for k in range(8):
        for gl in range(2):
            g = 2 * k + gl
            ch0 = 32 * (k % 4) + 16 * gl
            bt_re[:, k, ch0:ch0 + 16, 64 * gl:64 * gl + 64] = b_re[:, g].transpose(0, 2, 1)
            bt_im[:, k, ch0:ch0 + 16, 64 * gl:64 * gl + 64] = b_im[:, g].transpose(0, 2, 1)
            ct_re[:, k, 64 * gl:64 * gl + 64, ch0:ch0 + 16] = c_re[:, g].transpose(0, 2, 1)
            ct_im[:, k, 64 * gl:64 * gl + 64, ch0:ch0 + 16] = c_im[:, g].transpose(0, 2, 1)
    sh["bt_re"], sh["bt_im"], sh["ct_re"], sh["ct_im"] = bt_re, bt_im, ct_re, ct_im

    def cols2(a):
        return np.ascontiguousarray(a.reshape(NL, 2, 128).transpose(0, 2, 1))
    sh["d_col"] = cols2(f("ssm_d")); sh["bglu_col"] = cols2(f("ssm_b_glu")); sh["vgain_col"] = cols2(f("gmlp_v_g"))
    sh["wst"] = np.ascontiguousarray(f("gmlp_w_s").transpose(0, 1, 3, 2))
    bs = f("gmlp_b_s")
    sh["bs_b"] = np.ascontiguousarray(np.repeat(bs[:, :, None, :], 64, axis=2).reshape(NL, 2, 128, 128))
    qg, kg = f("q_norm_g"), f("k_norm_g")
    sh["qk_cols"] = np.ascontiguousarray(np.stack(
        [np.tile(qg, (1, 2)), np.tile(np.roll(qg, -32, axis=1), (1, 2)),
         np.tile(kg, (1, 2)), np.tile(np.roll(kg, -32, axis=1), (1, 2))], axis=2))
    sh["lam_rows"] = np.ascontiguousarray(np.stack([f("lambda_q1"), f("lambda_k1"), f("lambda_q2"), f("lambda_k2")], axis=1))
    sh["subln_col"] = np.ascontiguousarray(f("subln_g")[:, :, None])
    cwv = f("conv_w")
    sh["cw_col"] = np.ascontiguousarray(cwv.reshape(NL, 3, 44, 128).transpose(0, 3, 2, 1))
    sh["cb_col"] = np.ascontiguousarray(f("conv_b").reshape(NL, 44, 128).transpose(0, 2, 1))
    sh["c_ident"] = _bf(np.eye(128, dtype=np.float32))
    rmat = np.zeros((128, 128), np.float32)
    for m in range(128):
        if (m % 64) < 32:
            rmat[m + 32, m] = -1.0
        else:
            rmat[m - 32, m] = 1.0
    sh["c_rm"] = _bf(rmat)
    blk = np.zeros((128, 128), np.float32)
    blk[:64, :64] = 1.0 / 64
    blk[64:, 64:] = 1.0 / 64
    sh["c_mblk"] = _bf(blk)
    sh["c_ones"] = _bf(np.ones((128, 128), np.float32))
    jj = np.arange(128)[:, None] // 64
    ii = np.arange(128)[None, :] // 64
    msk = (jj <= ii).astype(np.float32)
    sh["c_mask"] = msk
    sh["c_maskb"] = _bf(msk)
    half = 32
    inv = (10000.0 ** (-np.arange(half, dtype=np.float32) / half)).astype(np.float32)
    ang = np.arange(S, dtype=np.float32)[:, None] * inv[None, :]
    ang = np.concatenate([ang, ang], axis=-1)
    cos = np.cos(ang).astype(np.float32).T
    sin = np.sin(ang).astype(np.float32).T
    sh["c_cos"] = np.ascontiguousarray(np.concatenate([cos, cos], axis=0))
    sh["c_sin"] = np.ascontiguousarray(np.concatenate([sin, sin], axis=0))
    sh["c_iota"] = np.ascontiguousarray(np.broadcast_to(np.arange(513, dtype=np.float32)[None, :], (128, 513)))
    return sh


_NC_CACHE = {}


def kernel(**inputs):
    x = np.asarray(inputs["x"], dtype=np.float32)
    sh = prep_shared(inputs)
    if "nc" not in _NC_CACHE:
        _NC_CACHE["nc"] = build()
    nc = _NC_CACHE["nc"]
    in_maps = []
    for c in range(8):
        m = dict(sh)
        m["x"] = np.ascontiguousarray(x[c])
        in_maps.append(m)
    res = run_bass_kernel_spmd(nc, in_maps, core_ids=list(range(8)))
    return np.stack([np.asarray(r["out"], dtype=np.float32) for r in res.results], axis=0)
```
